# Optimizing a Trainium2 kernel written in Bass

```python
import math
import jax, jax.numpy as jnp
from jax import lax
import numpy as np

D_MODEL = 1024
BATCH = 16
SEQ = 2048
DEPTH = 4

HEAD_DIM = 64
EPS = 1e-6
NEG_INF = -1e30
A_HEADS = 8
A_BRANCHES = ((128, 1), (512, 4), (2048, 16))
T5_BUCKETS = 32
T5_MAX_DIST = 1024
B_HEADS = 8
B_Q_RANK = 768
B_KV_RANK = 256
B_NOPE = 64
B_ROPE = 32
B_V = 64
ROPE_BASE = 10000.0
ATTN_BLOCK = 128
C_GROUPS = 8
C_GROUP_W = 64
C_CHUNK = 128
C_WIDTH = C_GROUPS * C_GROUP_W
D_HEADS = 8
GRID_W = 64
NA_ROWS = 8
NA_COLS = 16
FFN_HIDDEN = -(-8 * D_MODEL // (3 * 256)) * 256
A_WIDTH = A_HEADS * HEAD_DIM
D_WIDTH = D_HEADS * HEAD_DIM
EVEN_IN = 3 * A_WIDTH + B_Q_RANK + B_KV_RANK + B_ROPE
EVEN_MIX = A_WIDTH + B_HEADS * B_V
ODD_IN = 2 * C_WIDTH + 3 * D_WIDTH
ODD_MIX = C_WIDTH + D_WIDTH
N_EVEN = (DEPTH + 1) // 2
N_ODD = DEPTH // 2

kernel_name = 'hybrid_dilated_mla_gmlp_natten_encoder'


def _rmsnorm(x, g):
    xf = x.astype(jnp.float32)
    y = xf * lax.rsqrt(jnp.mean(xf * xf, -1, keepdims=True) + EPS) * g.astype(jnp.float32)
    return y.astype(x.dtype)


def _t5_bucket(rel):
    nb = T5_BUCKETS // 2
    max_exact = nb // 2
    n = np.abs(rel)
    large = max_exact + (np.log(np.maximum(n, 1) / max_exact) / math.log(T5_MAX_DIST / max_exact) * (nb - max_exact)).astype(np.int64)
    large = np.minimum(large, nb - 1)
    return ((rel > 0) * nb + np.where(n < max_exact, n, large)).astype(np.int32)


def _dilated_branch(q, k, v, t5_table, window, dil):
    B, S, H, hd = q.shape
    half = (window // 2) // dil
    L = S // dil
    nb = -(-L // half)
    Lp = nb * half

    def sub(t):
        return t.reshape(B, L, dil, H, hd).transpose(0, 2, 1, 3, 4)

    qs = jnp.pad(sub(q), ((0, 0), (0, 0), (0, Lp - L), (0, 0), (0, 0))).reshape(B, dil, nb, half, H, hd)

    def win(t):
        tp = jnp.pad(sub(t), ((0, 0), (0, 0), (half, Lp - L + half), (0, 0), (0, 0))).reshape(B, dil, nb + 2, half, H, hd)
        return jnp.concatenate([tp[:, :, 0:nb], tp[:, :, 1:nb + 1], tp[:, :, 2:nb + 2]], axis=3)

    kw, vw = win(k), win(v)
    qi = np.arange(half)[:, None]
    kj = np.arange(3 * half)[None, :]
    off = kj - half - qi
    key_idx = np.arange(nb)[:, None, None] * half - half + kj[None]
    valid = (np.abs(off) <= half)[None] & (key_idx >= 0) & (key_idx < L)
    bias = jnp.transpose(t5_table[_t5_bucket(off * dil)], (2, 0, 1)).astype(jnp.float32)
    s = jnp.einsum('bgnqhd,bgnkhd->bgnhqk', qs, kw).astype(jnp.float32) * (hd ** -0.5) + bias
    s = jnp.where(valid[None, None, :, None], s, NEG_INF)
    m = jnp.max(s, -1, keepdims=True)
    p = jnp.exp(s - m)
    den = jnp.sum(p, -1)
    o = jnp.einsum('bgnhqk,bgnkhd->bgnqhd', p, vw.astype(jnp.float32)) / jnp.transpose(den, (0, 1, 2, 4, 3))[..., None]
    lse = m[..., 0] + jnp.log(den)
    o = o.reshape(B, dil, Lp, H, hd)[:, :, :L].transpose(0, 2, 1, 3, 4).reshape(B, S, H, hd)
    lse = jnp.transpose(lse, (0, 1, 2, 4, 3)).reshape(B, dil, Lp, H)[:, :, :L].transpose(0, 2, 1, 3).reshape(B, S, H)
    return o, lse


def _dilated_mixture(q, k, v, t5_table):
    B, S, H, hd = q.shape
    outs, lses = zip(*[_dilated_branch(q, k, v, t5_table, w, d) for (w, d) in A_BRANCHES])
    wts = jax.nn.softmax(jnp.stack(lses, 0), axis=0)
    o = jnp.sum(wts[..., None] * jnp.stack(outs, 0), 0)
    return o.astype(q.dtype).reshape(B, S, H * hd)


def _rope_tables(S):
    pos = jnp.arange(S, dtype=jnp.float32)
    inv = 1.0 / (ROPE_BASE ** (jnp.arange(0, B_ROPE, 2, dtype=jnp.float32) / B_ROPE))
    ang = pos[:, None] * inv[None, :]
    return jnp.cos(ang), jnp.sin(ang)


def _apply_rope(x, cos, sin):
    hf = B_ROPE // 2
    shape = (1, x.shape[1]) + (1,) * (x.ndim - 3) + (hf,)
    c, s = cos.reshape(shape), sin.reshape(shape)
    xf = x.astype(jnp.float32)
    x1, x2 = xf[..., :hf], xf[..., hf:]
    return jnp.concatenate([x1 * c - x2 * s, x1 * s + x2 * c], -1).astype(x.dtype)


def _mla(c_q, c_kv, k_pe, q_gain, kv_gain, w_uq, w_ukv, cos, sin):
    B, S, _ = c_q.shape
    q = (_rmsnorm(c_q, q_gain) @ w_uq).reshape(B, S, B_HEADS, B_NOPE + B_ROPE)
    kv = (_rmsnorm(c_kv, kv_gain) @ w_ukv).reshape(B, S, B_HEADS, B_NOPE + B_V)
    q = jnp.concatenate([q[..., :B_NOPE], _apply_rope(q[..., B_NOPE:], cos, sin)], -1)
    k_pe = _apply_rope(k_pe, cos, sin)
    k = jnp.concatenate([kv[..., :B_NOPE], jnp.broadcast_to(k_pe[:, :, None, :], (B, S, B_HEADS, B_ROPE))], -1)
    v = kv[..., B_NOPE:]
    scale = (B_NOPE + B_ROPE) ** -0.5
    nq = S // ATTN_BLOCK
    qb = q.reshape(B, nq, ATTN_BLOCK, B_HEADS, B_NOPE + B_ROPE).transpose(1, 0, 2, 3, 4)

    def block(qblk):
        s = jnp.einsum('bqhd,bkhd->bhqk', qblk, k).astype(jnp.float32) * scale
        p = jax.nn.softmax(s, -1)
        return jnp.einsum('bhqk,bkhd->bqhd', p, v.astype(jnp.float32)).astype(v.dtype)

    o = lax.map(block, qb)
    return o.transpose(1, 0, 2, 3, 4).reshape(B, S, B_HEADS * B_V)


def _spatial_gating(z_in, v_gain, w_s, b_s):
    B, S, _ = z_in.shape
    z = jax.nn.gelu(z_in.astype(jnp.float32))
    u, v = z[..., :C_WIDTH], z[..., C_WIDTH:]
    mu = jnp.mean(v, -1, keepdims=True)
    var = jnp.mean((v - mu) ** 2, -1, keepdims=True)
    vn = (v - mu) * lax.rsqrt(var + EPS) * v_gain.astype(jnp.float32)
    vc = vn.reshape(B, S // C_CHUNK, C_CHUNK, C_GROUPS, C_GROUP_W)
    sv = jnp.einsum('gij,bnjgc->bnigc', w_s.astype(jnp.float32), vc) + jnp.transpose(b_s.astype(jnp.float32))[None, None, :, :, None]
    return (u * sv.reshape(B, S, C_WIDTH)).astype(z_in.dtype)


def _neighbourhood_attention(q, k, v, rpb):
    B, S, H, hd = q.shape
    rows = S // GRID_W
    kr = min(NA_ROWS, rows)
    n_cb = GRID_W // NA_COLS
    kcw = 2 * NA_COLS
    qcol = np.arange(GRID_W).reshape(n_cb, NA_COLS)
    qstart = np.clip(qcol - NA_COLS // 2, 0, GRID_W - NA_COLS)
    kb_start = np.clip(np.arange(n_cb) * NA_COLS - NA_COLS // 2, 0, GRID_W - kcw)
    kcol = kb_start[:, None] + np.arange(kcw)[None, :]
    col_mask = (kcol[:, None, :] >= qstart[..., None]) & (kcol[:, None, :] < qstart[..., None] + NA_COLS)
    dc = np.clip(kcol[:, None, :] - qcol[..., None] + NA_COLS - 1, 0, 2 * NA_COLS - 2)
    qg = q.reshape(B, rows, GRID_W, H, hd)
    kg = k.reshape(B, rows, GRID_W, H, hd)
    vg = v.reshape(B, rows, GRID_W, H, hd)
    scale = hd ** -0.5

    def row_fn(args):
        qr, i = args
        r0 = jnp.clip(i - kr // 2, 0, rows - kr)
        krows = lax.dynamic_slice_in_dim(kg, r0, kr, axis=1)
        vrows = lax.dynamic_slice_in_dim(vg, r0, kr, axis=1)
        kblk = jnp.stack([krows[:, :, int(s0):int(s0) + kcw] for s0 in kb_start], axis=2)
        vblk = jnp.stack([vrows[:, :, int(s0):int(s0) + kcw] for s0 in kb_start], axis=2)
        qb = qr.reshape(B, n_cb, NA_COLS, H, hd)
        s = jnp.einsum('bcqhd,bacmhd->bhcqam', qb, kblk).astype(jnp.float32) * scale
        dr = r0 + jnp.arange(kr) - i + NA_ROWS - 1
        bias = rpb[:, dr[:, None, None, None], dc[None]].astype(jnp.float32)
        s = s + jnp.transpose(bias, (0, 2, 3, 1, 4))[None]
        s = jnp.where(col_mask[:, :, None, :], s, NEG_INF)
        p = jax.nn.softmax(s.reshape(B, H, n_cb, NA_COLS, kr * kcw), -1).reshape(s.shape)
        o = jnp.einsum('bhcqam,bacmhd->bcqhd', p, vblk.astype(jnp.float32))
        return o.reshape(B, GRID_W, H, hd).astype(q.dtype)

    out = lax.map(row_fn, (qg.transpose(1, 0, 2, 3, 4), jnp.arange(rows, dtype=jnp.int32)))
    return out.transpose(1, 0, 2, 3, 4).reshape(B, S, H * hd)


def setup_inputs(seed: int = 0) -> dict:
    key = jax.random.key(seed)
    ks = jax.random.split(key, 20)

    def nrm(k, shape, scale):
        return jax.random.normal(k, shape, jnp.float32) * scale

    def gain(k, shape):
        return 1.0 + 0.05 * jax.random.normal(k, shape, jnp.float32)

    return {
        'x': nrm(ks[0], (BATCH, SEQ, D_MODEL), 1.0),
        't5_bias': nrm(ks[1], (T5_BUCKETS, A_HEADS), 0.2),
        'norm_mix': gain(ks[2], (DEPTH, D_MODEL)),
        'norm_ffn': gain(ks[3], (DEPTH, D_MODEL)),
        'ev_w_in': nrm(ks[4], (N_EVEN, D_MODEL, EVEN_IN), D_MODEL ** -0.5),
        'ev_q_gain': gain(ks[5], (N_EVEN, B_Q_RANK)),
        'ev_kv_gain': gain(ks[6], (N_EVEN, B_KV_RANK)),
        'ev_w_uq': nrm(ks[7], (N_EVEN, B_Q_RANK, B_HEADS * (B_NOPE + B_ROPE)), B_Q_RANK ** -0.5),
        'ev_w_ukv': nrm(ks[8], (N_EVEN, B_KV_RANK, B_HEADS * (B_NOPE + B_V)), B_KV_RANK ** -0.5),
        'ev_w_out': nrm(ks[9], (N_EVEN, EVEN_MIX, D_MODEL), EVEN_MIX ** -0.5),
        'od_w_in': nrm(ks[10], (N_ODD, D_MODEL, ODD_IN), D_MODEL ** -0.5),
        'od_v_gain': gain(ks[11], (N_ODD, C_WIDTH)),
        'od_w_s': nrm(ks[12], (N_ODD, C_GROUPS, C_CHUNK, C_CHUNK), C_CHUNK ** -0.5),
        'od_b_s': nrm(ks[13], (N_ODD, C_GROUPS, C_CHUNK), 0.1),
        'od_rpb': nrm(ks[14], (N_ODD, D_HEADS, 2 * NA_ROWS - 1, 2 * NA_COLS - 1), 0.2),
        'od_w_out': nrm(ks[15], (N_ODD, ODD_MIX, D_MODEL), ODD_MIX ** -0.5),
        'ffn_w_gu': nrm(ks[16], (DEPTH, D_MODEL, 2 * FFN_HIDDEN), D_MODEL ** -0.5),
        'ffn_w_down': nrm(ks[17], (DEPTH, FFN_HIDDEN, D_MODEL), FFN_HIDDEN ** -0.5),
        'final_gain': gain(ks[18], (D_MODEL,)),
    }


def reference(x, t5_bias, norm_mix, norm_ffn, ev_w_in, ev_q_gain, ev_kv_gain, ev_w_uq, ev_w_ukv, ev_w_out,
              od_w_in, od_v_gain, od_w_s, od_b_s, od_rpb, od_w_out, ffn_w_gu, ffn_w_down, final_gain):
    B, S, _ = x.shape
    cos, sin = _rope_tables(S)
    for layer in range(DEPTH):
        h = _rmsnorm(x, norm_mix[layer])
        j = layer // 2
        if layer % 2 == 0:
            p = h @ ev_w_in[j]
            qa, ka, va = [p[..., i * A_WIDTH:(i + 1) * A_WIDTH].reshape(B, S, A_HEADS, HEAD_DIM) for i in range(3)]
            o0 = 3 * A_WIDTH
            o1 = o0 + B_Q_RANK
            o2 = o1 + B_KV_RANK
            a_out = _dilated_mixture(qa, ka, va, t5_bias)
            b_out = _mla(p[..., o0:o1], p[..., o1:o2], p[..., o2:o2 + B_ROPE], ev_q_gain[j], ev_kv_gain[j],
                         ev_w_uq[j], ev_w_ukv[j], cos, sin)
            x = x + jnp.concatenate([a_out, b_out], -1) @ ev_w_out[j]
        else:
            p = h @ od_w_in[j]
            c_out = _spatial_gating(p[..., :2 * C_WIDTH], od_v_gain[j], od_w_s[j], od_b_s[j])
            base = 2 * C_WIDTH
            qd, kd, vd = [p[..., base + i * D_WIDTH:base + (i + 1) * D_WIDTH].reshape(B, S, D_HEADS, HEAD_DIM) for i in range(3)]
            d_out = _neighbourhood_attention(qd, kd, vd, od_rpb[j])
            x = x + jnp.concatenate([c_out, d_out], -1) @ od_w_out[j]
        h = _rmsnorm(x, norm_ffn[layer])
        gu = h @ ffn_w_gu[layer]
        x = x + (jax.nn.silu(gu[..., :FFN_HIDDEN]) * gu[..., FFN_HIDDEN:]) @ ffn_w_down[layer]
    return _rmsnorm(x, final_gain)
```

```python
import numpy as np
from contextlib import ExitStack
import concourse.bass as bass
import concourse.mybir as mybir
from concourse.bass_utils import run_bass_kernel_spmd

F32 = mybir.dt.float32
BF16 = mybir.dt.bfloat16
AF = mybir.ActivationFunctionType
ALU = mybir.AluOpType

D = 1024
S = 2048
NCORES = 8
SEQ_PER_CORE = 2
FH = 2816
EPS = 1e-6
ENGS = ("pe", "act", "dve", "pool", "sp")


class Res:
    __slots__ = ("name", "last_w", "readers", "sem", "ndma")

    def __init__(self, name):
        self.name = name
        self.last_w = None
        self.readers = {}
        self.sem = None
        self.ndma = 0


class Op:
    __slots__ = ("eng", "fn", "deps", "signal", "val", "dma", "ndma", "key", "seq")


class Prog:
    def __init__(self, nc, stack):
        self.nc = nc
        self.stack = stack
        self.ops = {e: [] for e in ENGS}
        self.dma_res = []
        self.nres = 0

    def res(self, name="r"):
        self.nres += 1
        return Res(f"{name}{self.nres}")

    def add(self, eng, fn, reads=(), writes=(), dma=None, ndma=1):
        op = Op()
        self.seq = getattr(self, "seq", 0) + 1
        op.seq = self.seq
        op.eng = eng
        op.fn = fn
        op.signal = False
        op.val = None
        op.dma = dma
        op.ndma = ndma
        deps = {}
        for r in reads:
            if r.last_w is not None:
                deps[id(r.last_w)] = r.last_w
        for w in writes:
            if w.readers:
                for o in w.readers.values():
                    deps[id(o)] = o
            elif w.last_w is not None and not (eng == "pe" and w.last_w.eng == "pe"):
                deps[id(w.last_w)] = w.last_w
        if dma is not None:
            if dma.sem is None:
                dma.sem = Res("box")
                dma.sem.sem = True
                self.dma_res.append(dma.sem)
            elif dma.sem.sem is None:
                dma.sem.sem = True
                self.dma_res.append(dma.sem)
            box = dma.sem
            box.ndma += ndma
            op.val = 16 * box.ndma
            op.key = ("d", id(box))
            op.signal = True
        else:
            op.key = ("e", eng)
        for r in reads:
            r.readers[op.key] = op
        for w in writes:
            w.last_w = op
            w.readers = {}
        deps.pop(id(op), None)
        op.deps = list(deps.values())
        for d in op.deps:
            d.signal = True
        self.ops[eng].append(op)
        return op

    def emit(self, final_waits=()):
        nc = self.nc
        sems = {}
        for e in ENGS:
            sems[("e", e)] = self.stack.enter_context(nc.semaphore(f"s_{e}"))
            c = 0
            for op in self.ops[e]:
                if op.dma is None and op.signal:
                    c += 1
                    op.val = c
        for i, r in enumerate(self.dma_res):
            sems[("d", id(r))] = self.stack.enter_context(nc.semaphore(f"d{i}_{r.name}"))
        fin = []
        for r in final_waits:
            if r.last_w is not None:
                fin.append(r.last_w)
        block = self.stack.enter_context(nc.Block())

        def run(e, eng):
            waited = {}
            for op in self.ops[e]:
                for d in op.deps:
                    if waited.get(d.key, 0) < d.val:
                        eng.wait_ge(sems[d.key], d.val)
                        waited[d.key] = d.val
                ins = op.fn(eng)
                if op.dma is not None:
                    if not isinstance(ins, (list, tuple)):
                        ins = [ins]
                    assert len(ins) == op.ndma, (len(ins), op.ndma)
                    for i in ins:
                        i.then_inc(sems[op.key], 16)
                elif op.signal:
                    if isinstance(ins, (list, tuple)):
                        ins = ins[-1]
                    ins.then_inc(sems[op.key], 1)
            if e == "sp":
                for d in fin:
                    if waited.get(d.key, 0) < d.val:
                        eng.wait_ge(sems[d.key], d.val)
                        waited[d.key] = d.val

        @block.tensor
        def _(eng):
            run("pe", eng)

        @block.scalar
        def _(eng):
            run("act", eng)

        @block.vector
        def _(eng):
            run("dve", eng)

        @block.gpsimd
        def _(eng):
            run("pool", eng)

        @block.sync
        def _(eng):
            run("sp", eng)


class Ring:
    def __init__(self, C, name, shape, dtype, n):
        self.bufs = [C.ws(name, shape, dtype) for _ in range(n)]
        self.i = 0

    def next(self):
        b = self.bufs[self.i % len(self.bufs)]
        self.i += 1
        return b


class ResRing:
    def __init__(self, items):
        self.items = items
        self.i = 0

    def next(self):
        b = self.items[self.i % len(self.items)]
        self.i += 1
        return b


def _bind(f, *a, **k):
    def g(e):
        return f(e, *a, **k)
    return g


class Ctx:
    def __init__(self, nc, stack, T):
        self.nc = nc
        self.T = T
        self.P = Prog(nc, stack)
        P = self.P
        self.stack = stack
        self.ps = []
        for i in range(8):
            t = stack.enter_context(nc.psum_tensor(f"ps{i}", [128, 512], F32))
            self.ps.append((t, P.res("ps")))
        self.psA = ResRing(self.ps[0:4])
        self.psB = ResRing(self.ps[4:6])
        self.psF = ResRing(self.ps[6:8])
        self.junk_ps = self.ps[7]
        self.junk_rhs = stack.enter_context(nc.sbuf_tensor("junk_rhs", [128, 512], BF16))
        self.junk_r = P.res("junk")
        P.add("dve", lambda e: e.memset(self.junk_rhs[:, :], 0.0), writes=[self.junk_r])
        self.psall = ResRing(self.ps)
        self.psO = ResRing(self.ps[4:6])
        self.psD = ResRing(self.ps[6:8])
        self.AW = 22528
        self.arenas = []
        for i in range(2):
            t = stack.enter_context(nc.sbuf_tensor(f"arena{i}", [128, self.AW], BF16))
            self.arenas.append((t, P.res("arena")))
        self.arena_i = 0
        self.WS = 29440
        self.wst = stack.enter_context(nc.sbuf_tensor("wspace", [128, self.WS], F32))
        self.ws_off = 0
        self.ws_live = []
        self.sem_boxes = {}
        self.ones = stack.enter_context(nc.sbuf_tensor("ones", [128, 128], BF16))
        self.ones_r = P.res("ones")
        self.eps_t = stack.enter_context(nc.sbuf_tensor("eps_t", [128, 8], F32))
        self.ones32 = stack.enter_context(nc.sbuf_tensor("ones32", [128, 64], F32))
        self.ident = stack.enter_context(nc.sbuf_tensor("ident", [128, 128], BF16))
        self.ident_r = P.res("ident")
        self.rotm = stack.enter_context(nc.sbuf_tensor("rotm", [128, 96], BF16))
        self.rotm_r = P.res("rotm")

        def init_consts(e):
            e.memset(self.ones[:, :], 1.0)
            e.memset(self.ones32[:, :], 1.0)
            e.memset(self.eps_t[:, 0:1], EPS)
            return e.memset(self.eps_t[:, 1:2], 96.0 * EPS)
        P.add("dve", init_consts, writes=[self.ones_r])

    def next_arena(self):
        a = self.arenas[self.arena_i % 2]
        self.arena_i += 1
        return a

    def ws_reset(self):
        self.ws_off = 0

    def ws(self, name, shape, dtype):
        n = 1
        for d in shape[1:]:
            n *= d
        esz = 4 if dtype == F32 else 2
        words = (n * esz + 3) // 4
        words = (words + 7) // 8 * 8
        st, en = self.ws_off, self.ws_off + words
        assert en <= self.WS, (name, en, self.WS)
        self.ws_off = en
        r = self.P.res(name)
        bk = (name, st)
        if bk not in self.sem_boxes:
            self.sem_boxes[bk] = Res("box")
        r.sem = self.sem_boxes[bk]
        keep = []
        for (a, b, o) in self.ws_live:
            if b <= st or a >= en:
                keep.append((a, b, o))
                continue
            cands = list(o.readers.values())
            if o.last_w is not None:
                cands.append(o.last_w)
            for op in cands:
                cur = r.readers.get(op.key)
                if cur is None or cur.seq < op.seq:
                    r.readers[op.key] = op
            if a < st:
                keep.append((a, st, o))
            if b > en:
                keep.append((en, b, o))
        keep.append((st, en, r))
        self.ws_live = keep
        v = self.wst[0:shape[0], st:en]
        if dtype != F32:
            v = v.bitcast(dtype)
        v = v[:, 0:n]
        if len(shape) == 3:
            v = v.rearrange("p (a b) -> p a b", a=shape[1])
        elif len(shape) == 4:
            v = v.rearrange("p (a b c) -> p a b c", a=shape[1], b=shape[2])
        return v, r

    def dram(self, name, shape, dtype):
        return self.nc.dram_tensor(name, list(shape), dtype).ap()

    def load_weight(self, arena, w_ap, K, col_slices, extra_reads=(), extra=0):
        at, ar = arena
        KC = (K + 127) // 128
        ncols = sum(c1 - c0 for c0, c1 in col_slices) + extra
        assert KC * ncols <= self.AW
        view = at[:, 0:KC * ncols].rearrange("p (k n) -> p k n", k=KC)

        def fn(e):
            ins = []
            for k in range(KC):
                rows = min(128, K - k * 128)
                o = 0
                for (c0, c1) in col_slices:
                    ins.append(e.dma_start(out=view[0:rows, k, o:o + (c1 - c0)],
                                           in_=w_ap[k * 128:k * 128 + rows, c0:c1]))
                    o += c1 - c0
            return ins
        self.P.add("pool", fn, reads=list(extra_reads), writes=[ar], dma=ar,
                   ndma=KC * len(col_slices))
        return view


def phase_norm(C, x_ap, x_res, g_sb, g_res, gcol, out_ap, out_res, out_dtype, rings):
    P = C.P
    T = C.T
    NT = T // 512
    xr_ring, sq_ring, rs_ring, h_ring = rings
    xv = x_ap.rearrange("(c p) t -> p c t", p=128)
    ov = out_ap.rearrange("(c p) t -> p c t", p=128)

    def load(t):
        xt, xr = xr_ring.next()
        P.add("sp", lambda e: e.dma_start(out=xt[:, :, :], in_=xv[:, :, t * 512:(t + 1) * 512]),
              reads=[x_res[t]], writes=[xr], dma=xr)
        return xt, xr

    def stats(xt, xr):
        sq, sqr = sq_ring.next()
        P.add("act", lambda e: e.activation(out=sq[:, :, :], in_=xt[:, :, :], func=AF.Square), reads=[xr], writes=[sqr])
        ps, pr = C.psall.next()

        def mm(e):
            last = None
            for c in range(8):
                last = e.matmul(ps[:, :], lhsT=C.ones[:, :], rhs=sq[:, c, :], start=(c == 0), stop=(c == 7))
            return last
        P.add("pe", mm, reads=[sqr, C.ones_r], writes=[pr])
        return ps, pr

    def apply(t, xt, xr, ps, pr):
        rs, rsr = rs_ring.next()
        P.add("act", lambda e: e.activation(out=rs[:, :], in_=ps[:, :], func=AF.Sqrt, bias=C.eps_t[:, 0:1], scale=1.0 / D),
              reads=[pr, C.ones_r], writes=[rsr])
        P.add("dve", lambda e: e.reciprocal(out=rs[:, :], in_=rs[:, :]), reads=[rsr], writes=[rsr])
        ht, hr = h_ring.next()

        def app(e, c0, c1):
            last = None
            for c in range(c0, c1):
                last = e.scalar_tensor_tensor(out=ht[:, c, :], in0=xt[:, c, :],
                                              scalar=g_sb[:, gcol + c:gcol + c + 1], in1=rs[:, :],
                                              op0=ALU.mult, op1=ALU.mult)
            return last
        P.add("dve", _bind(app, 0, 8), reads=[xr, rsr, g_res], writes=[hr])
        P.add("sp", lambda e: e.dma_start(out=ov[:, :, t * 512:(t + 1) * 512], in_=ht[:, :, :]),
              reads=[hr], writes=[out_res[t]], dma=hr)
    split_res = {}
    tmp_pool = [C.ws("ntmp", [128, 512], F32) for _ in range(2)]
    tiles = [load(0)]
    if NT > 1:
        tiles.append(load(1))
    st = [stats(*tiles[0])]
    for t in range(NT):
        if t + 2 < NT:
            tiles.append(load(t + 2))
        if t + 1 < NT:
            st.append(stats(*tiles[t + 1]))
        apply(t, tiles[t][0], tiles[t][1], st[t][0], st[t][1])


def phase_ffn_gu(C, h_ap, h_res, wgu_ap, a_ap, a_res, rings):
    P = C.P
    T = C.T
    NT = T // 512
    x_ring, sg_ring, a_ring = rings
    hv = h_ap.rearrange("(c p) t -> p c t", p=128)
    HH = FH // 2
    for hh in range(2):
        arena = C.next_arena()
        W = C.load_weight(arena, wgu_ap, D, [(hh * HH, (hh + 1) * HH), (FH + hh * HH, FH + (hh + 1) * HH)])
        av = a_ap[hh * HH:(hh + 1) * HH, :].rearrange("(c p) t -> p c t", p=128)

        def load(t):
            xt, xr = x_ring.next()
            P.add("sp", lambda e: e.dma_start(out=xt[:, :, :], in_=hv[:, :, t * 512:(t + 1) * 512]),
                  reads=[h_res[t]], writes=[xr], dma=xr)
            return xt, xr
        nxt = load(0)
        for t in range(NT):
            xt, xr = nxt
            if t + 1 < NT:
                nxt = load(t + 1)
            at, atr = a_ring.next()
            for j in range(11):
                psg, pgr = C.psall.next()
                psu, pur = C.psall.next()

                def mm(e, ps, c0, xt=xt, W=W):
                    last = None
                    for k in range(8):
                        last = e.matmul(ps[:, :], lhsT=W[:, k, c0:c0 + 128], rhs=xt[:, k, :],
                                        start=(k == 0), stop=(k == 7))
                    return last
                P.add("pe", _bind(mm, psg, j * 128), reads=[xr, arena[1]], writes=[pgr])
                P.add("pe", _bind(mm, psu, HH + j * 128), reads=[xr, arena[1]], writes=[pur])
                sg, sgr = sg_ring.next()
                P.add("act", lambda e, sg=sg, psg=psg: e.activation(out=sg[:, :], in_=psg[:, :], func=AF.Silu),
                      reads=[pgr], writes=[sgr])
                P.add("dve", lambda e, at=at, j=j, sg=sg, psu=psu: e.tensor_tensor(
                    out=at[:, j, :], in0=psu[:, :], in1=sg[:, :], op=ALU.mult),
                    reads=[pur, sgr], writes=[atr])
            P.add("sp", lambda e, at=at, t=t, av=av: e.dma_start(out=av[:, :, t * 512:(t + 1) * 512], in_=at[:, :, :]),
                  reads=[atr], writes=[a_res[hh][t]], dma=atr)


def phase_proj_residual(C, in_ap, in_res_fn, K, w_ap, x_src_ap, x_src_res, x_dst_ap, x_dst_res, rings,
                        fuse_norm=None):
    P = C.P
    T = C.T
    NT = T // 512
    KC = K // 128
    in_ring, x_ring = rings
    arena = C.next_arena()
    W = C.load_weight(arena, w_ap, K, [(0, D)])
    iv = in_ap.rearrange("(c p) t -> p c t", p=128)
    xsv = x_src_ap.rearrange("(c p) t -> p c t", p=128)
    xdv = x_dst_ap.rearrange("(c p) t -> p c t", p=128)
    if fuse_norm is not None:
        g_sb, g_res, gcol, h_ap, h_res, (sq_ring, rs_ring, h_ring) = fuse_norm
        hv = h_ap.rearrange("(c p) t -> p c t", p=128)

    def load(t):
        it, ir = in_ring.next()
        P.add("sp", lambda e: e.dma_start(out=it[:, 0:KC, :], in_=iv[:, :, t * 512:(t + 1) * 512]),
              reads=in_res_fn(t), writes=[ir], dma=ir)
        xt, xr = x_ring.next()
        P.add("sp", lambda e: e.dma_start(out=xt[:, :, :], in_=xsv[:, :, t * 512:(t + 1) * 512]),
              reads=[x_src_res[t]], writes=[xr], dma=xr)
        return it, ir, xt, xr

    def norm_tail(t, xt, xr):
        sq, sqr = sq_ring.next()
        P.add("act", lambda e: e.activation(out=sq[:, :, :], in_=xt[:, :, :], func=AF.Square), reads=[xr], writes=[sqr])
        ps, pr = C.psall.next()

        def mm(e):
            last = None
            for c in range(8):
                last = e.matmul(ps[:, :], lhsT=C.ones[:, :], rhs=sq[:, c, :], start=(c == 0), stop=(c == 7))
            return last
        P.add("pe", mm, reads=[sqr, C.ones_r], writes=[pr])
        rs, rsr = rs_ring.next()
        P.add("act", lambda e: e.activation(out=rs[:, :], in_=ps[:, :], func=AF.Sqrt, bias=C.eps_t[:, 0:1], scale=1.0 / D),
              reads=[pr, C.ones_r], writes=[rsr])
        P.add("dve", lambda e: e.reciprocal(out=rs[:, :], in_=rs[:, :]), reads=[rsr], writes=[rsr])
        ht, hr = h_ring.next()

        def app(e):
            last = None
            for c in range(8):
                last = e.scalar_tensor_tensor(out=ht[:, c, :], in0=xt[:, c, :], scalar=g_sb[:, gcol + c:gcol + c + 1],
                                              in1=rs[:, :], op0=ALU.mult, op1=ALU.mult)
            return last
        P.add("dve", app, reads=[xr, rsr, g_res], writes=[hr])
        P.add("sp", lambda e: e.dma_start(out=hv[:, :, t * 512:(t + 1) * 512], in_=ht[:, :, :]),
              reads=[hr], writes=[h_res[t]], dma=hr)
    nxt = load(0)
    pend = None
    for t in range(NT):
        it, ir, xt, xr = nxt
        if t + 1 < NT:
            nxt = load(t + 1)
        for oc in range(8):
            ps, pr = C.psall.next()

            def mm(e, ps=ps, oc=oc, it=it):
                last = None
                for k in range(KC):
                    last = e.matmul(ps[:, :], lhsT=W[:, k, oc * 128:(oc + 1) * 128], rhs=it[:, k, :],
                                    start=(k == 0), stop=(k == KC - 1))
                return last
            P.add("pe", mm, reads=[ir, arena[1]], writes=[pr])
            P.add("dve", lambda e, xt=xt, oc=oc, ps=ps: e.tensor_tensor(
                out=xt[:, oc, :], in0=ps[:, :], in1=xt[:, oc, :], op=ALU.add),
                reads=[pr, xr], writes=[xr])
            if oc == 3 and pend is not None:
                norm_tail(*pend)
                pend = None
        P.add("sp", lambda e, xt=xt, t=t: e.dma_start(out=xdv[:, :, t * 512:(t + 1) * 512], in_=xt[:, :, :]),
              reads=[xr], writes=[x_dst_res[t]], dma=xr)
        if fuse_norm is not None:
            pend = (t, xt, xr)
    if pend is not None:
        norm_tail(*pend)


def phase_inproj(C, h_ap, h_res, W, w_res, fm_groups, tm_col0, pT_ap, p_res, vtok_ap, v_res, rings):
    P = C.P
    T = C.T
    NT = T // 512
    x_ring, st_ring, vt_ring = rings
    hv = h_ap.rearrange("(c p) t -> p c t", p=128)
    NG = len(fm_groups)
    pv = pT_ap[0:NG * 128, :].rearrange("(c p) t -> p c t", p=128)
    vv = vtok_ap.rearrange("(n p) c -> p n c", p=128) if tm_col0 is not None else None

    def load(t):
        xt, xr = x_ring.next()
        P.add("sp", lambda e: e.dma_start(out=xt[:, :, :], in_=hv[:, :, t * 512:(t + 1) * 512]),
              reads=[h_res[t]], writes=[xr], dma=xr)
        return xt, xr
    for (stb, stbr) in st_ring.bufs:
        P.add("dve", lambda e, stb=stb: e.memset(stb[:, :, :], 0.0), writes=[stbr])
    nxt = load(0)
    cnt = 0
    for t in range(NT):
        xt, xr = nxt
        if t + 1 < NT:
            nxt = load(t + 1)
        st, sr = st_ring.next()
        for g, (c0, ncol, scale) in enumerate(fm_groups):
            ps, pr = C.psall.next()

            def mm(e, ps=ps, c0=c0, ncol=ncol, xt=xt):
                last = None
                for k in range(8):
                    last = e.matmul(ps[0:ncol, :], lhsT=W[:, k, c0:c0 + ncol], rhs=xt[:, k, :],
                                    start=(k == 0), stop=(k == 7))
                return last
            P.add("pe", mm, reads=[xr, w_res], writes=[pr])
            cnt += 1
            if cnt % 2 == 0:
                P.add("act", lambda e, st=st, g=g, ps=ps, ncol=ncol, scale=scale: e.activation(
                    out=st[0:ncol, g, :], in_=ps[0:ncol, :], func=AF.Copy, scale=float(scale)),
                    reads=[pr], writes=[sr])
            else:
                P.add("dve", lambda e, st=st, g=g, ps=ps, ncol=ncol, scale=scale: e.tensor_scalar(
                    out=st[0:ncol, g, :], in0=ps[0:ncol, :], scalar1=float(scale), scalar2=None, op0=ALU.mult),
                    reads=[pr], writes=[sr])
        P.add("sp", lambda e, st=st, t=t: e.dma_start(out=pv[:, :, t * 512:(t + 1) * 512], in_=st[:, 0:NG, :]),
              reads=[sr], writes=[p_res[t]], dma=sr)
        if tm_col0 is not None:
            vt, vr = vt_ring.next()
            for sl in range(4):
                ps, pr = C.psall.next()

                def mmv(e, ps=ps, sl=sl, xt=xt):
                    last = None
                    for k in range(8):
                        last = e.matmul(ps[:, :], lhsT=xt[:, k, sl * 128:(sl + 1) * 128],
                                        rhs=W[:, k, tm_col0:tm_col0 + 512], start=(k == 0), stop=(k == 7))
                    return last
                P.add("pe", mmv, reads=[xr, w_res], writes=[pr])
                if sl % 2 == 0:
                    P.add("act", lambda e, vt=vt, sl=sl, ps=ps: e.activation(out=vt[:, sl, :], in_=ps[:, :], func=AF.Copy),
                          reads=[pr], writes=[vr])
                else:
                    P.add("dve", lambda e, vt=vt, sl=sl, ps=ps: e.tensor_copy(out=vt[:, sl, :], in_=ps[:, :]),
                          reads=[pr], writes=[vr])
            P.add("sp", lambda e, vt=vt, t=t: e.dma_start(out=vv[:, t * 4:(t + 1) * 4, :], in_=vt[:, :, :]),
                  reads=[vr], writes=[v_res[t]], dma=vr)


class AttnStream:
    def __init__(self, C, pt_ring, L=2, warm=0):
        self.warm = warm
        self.C = C
        self.pt_ring = pt_ring
        self.L = L
        self.cur = []
        self.cur_e = []
        self.cols = 0
        self.pending = []
        self.acc_bank = None
        self.acc_col = 512
        self.deferred = []

    def defer(self, fn, nb):
        self.deferred.append([nb, fn])

    def _tick(self):
        run = [d for d in self.deferred if d[0] <= 0]
        self.deferred = [d for d in self.deferred if d[0] > 0]
        for d in self.deferred:
            d[0] -= 1
        for d in run:
            d[1]()

    def begin_group(self, Nq, qT, q_reads, npairs, finalize):
        if self.acc_col + Nq > 512:
            self.acc_bank = self.C.psB.next()
            self.acc_col = 0
        bo, bor = self.acc_bank
        g = dict(Nq=Nq, qT=qT, q_reads=list(q_reads), n=npairs, fin=finalize, done=0, seen=0,
                 acc=bo, accr=bor, co=self.acc_col)
        self.acc_col += Nq
        return g

    def pair(self, g, kT, v, bias, reads, e=None):
        if self.cols + g["Nq"] > 512:
            self._flush()
        self.cur_e.append(e)
        self.cur.append((g, kT, v, bias, list(reads), self.cols, g["seen"]))
        g["seen"] += 1
        self.cols += g["Nq"]

    def _flush(self):
        if not self.cur:
            return
        C = self.C
        P = C.P
        batch, cols = self.cur, self.cols
        es = self.cur_e
        self.cur, self.cols, self.cur_e = [], 0, []
        st, sr = C.psA.next()
        pt, pr = self.pt_ring.next()
        rds = [C.ident_r]
        for (g, kT, v, bias, reads, c0, idx) in batch:
            rds += g["q_reads"] + reads

        def s1(e):
            last = None
            for (g, kT, v, bias, reads, c0, idx) in batch:
                Nq = g["Nq"]
                last = e.matmul(st[:, c0:c0 + Nq], lhsT=kT, rhs=g["qT"], start=True, stop=(bias is None))
                if bias is not None:
                    last = e.matmul(st[:, c0:c0 + Nq], lhsT=C.ident[:, :], rhs=bias, start=False, stop=True)
            return last
        P.add("pe", s1, reads=rds, writes=[sr])
        if self.warm:
            jt, jr = C.junk_ps
            nw = self.warm

            def wm(e):
                last = None
                for _ in range(nw):
                    last = e.matmul(jt[:, 0:512], lhsT=C.ident[:, :], rhs=C.junk_rhs[:, :], start=True, stop=True)
                return last
            P.add("pe", wm, reads=[C.ident_r, C.junk_r], writes=[jr])
        P.add("act", lambda e: e.activation(out=pt[:, 0:cols], in_=st[:, 0:cols], func=AF.Exp), reads=[sr], writes=[pr])
        if any(x is not None for x in es):
            assert all(x is not None for x in es)
            n = len(es)
            contiguous = all(es[i][0] is es[0][0] and es[i][1] == es[0][1] + i for i in range(n)) and \
                all(b[0]["Nq"] == 128 for b in batch)
            if contiguous:
                et, p0, er = es[0]
                P.add("dve", lambda e: e.tensor_tensor(out=pt[:, 0:cols].rearrange("p (a b) -> p a b", a=n),
                                                       in0=pt[:, 0:cols].rearrange("p (a b) -> p a b", a=n),
                                                       in1=et[:, p0:p0 + n, :], op=ALU.mult), reads=[pr, er], writes=[pr])
            else:
                for (et, p0, er), b in zip(es, batch):
                    c0, Nq = b[5], b[0]["Nq"]
                    P.add("dve", lambda e, et=et, p0=p0, c0=c0, Nq=Nq: e.tensor_tensor(out=pt[:, c0:c0 + Nq], in0=pt[:, c0:c0 + Nq],
                                                                                   in1=et[:, p0, :], op=ALU.mult), reads=[pr, er], writes=[pr])
        self.pending.append((batch, pt, pr))
        while len(self.pending) > self.L:
            self._stage2(self.pending.pop(0))
        self._tick()

    def _stage2(self, item):
        C = self.C
        P = C.P
        batch, pt, pr = item
        rds = [pr]
        wr = []
        for (g, kT, v, bias, reads, c0, idx) in batch:
            rds += reads
            if g["accr"] not in wr:
                wr.append(g["accr"])

        def s2(e):
            last = None
            for (g, kT, v, bias, reads, c0, idx) in batch:
                Nq = g["Nq"]
                last = e.matmul(g["acc"][0:65, g["co"]:g["co"] + Nq], lhsT=v, rhs=pt[:, c0:c0 + Nq],
                                start=(idx == 0), stop=(idx == g["n"] - 1))
            return last
        P.add("pe", s2, reads=rds, writes=wr)
        for (g, kT, v, bias, reads, c0, idx) in batch:
            g["done"] += 1
            if g["done"] == g["n"]:
                Nq = g["Nq"]
                g["fin"](g["acc"][0:65, g["co"]:g["co"] + Nq], [g["accr"]])

    def finish(self):
        self._flush()
        while self.pending:
            self._stage2(self.pending.pop(0))
        while self.deferred:
            for d in self.deferred:
                d[0] = 0
            self._tick()


def load_weight_at(C, arena, off, w_ap, K, c0, c1):
    at, ar = arena
    KC = (K + 127) // 128
    n = c1 - c0
    assert off + KC * n <= C.AW
    view = at[:, off:off + KC * n].rearrange("p (k n) -> p k n", k=KC)

    def fn(e):
        return [e.dma_start(out=view[:, k, :], in_=w_ap[k * 128:(k + 1) * 128, c0:c1]) for k in range(KC)]
    C.P.add("pool", fn, writes=[ar], dma=ar, ndma=KC)
    return view


def phase_mla_prep(C, pT_ap, p_res, w_uq_ap, w_ukv_ap, vec_sb, vec_res, qg_col, kvg_col, cos_ap, sin_ap,
                   qB_ap, kB_ap, vB_ap, qk_res, vb_res):
    P = C.P
    T = C.T
    NT = T // 512
    arena = C.next_arena()
    at, ar = arena
    Wq = load_weight_at(C, arena, 0, w_uq_ap, 768, 0, 768)
    Wkv = load_weight_at(C, arena, 2 * 4608, w_ukv_ap, 256, 0, 1024)

    def prep(e):
        last = None
        for k in range(6):
            last = e.tensor_scalar(out=Wq[:, k, :], in0=Wq[:, k, :], scalar1=vec_sb[:, qg_col + k:qg_col + k + 1],
                                   scalar2=None, op0=ALU.mult)
        for k in range(2):
            last = e.tensor_scalar(out=Wkv[:, k, :], in0=Wkv[:, k, :], scalar1=vec_sb[:, kvg_col + k:kvg_col + k + 1],
                                   scalar2=None, op0=ALU.mult)
        return last
    P.add("dve", prep, reads=[ar, vec_res], writes=[ar])
    C.ws_reset()
    cq_ring = Ring(C, "cq", [128, 6, 512], BF16, 2)
    ckv_ring = Ring(C, "ckv", [128, 2, 512], BF16, 2)
    kp_ring = Ring(C, "kp", [128, 2, 512], BF16, 2)
    cs_ring = Ring(C, "cs", [128, 2, 512], F32, 2)
    sqq_ring = Ring(C, "sqq", [128, 6, 512], BF16, 1)
    sqk_ring = Ring(C, "sqk", [128, 2, 512], BF16, 1)
    rq_ring = Ring(C, "rq", [128, 512], F32, 1)
    rk_ring = Ring(C, "rk", [128, 512], F32, 1)
    rkt_ring = Ring(C, "rkt", [128, 4], F32, 1)
    t1_ring = Ring(C, "t1", [128, 512], F32, 2)
    t2_ring = Ring(C, "t2", [128, 512], F32, 2)
    qst_ring = Ring(C, "qst", [128, 8, 512], BF16, 2)
    kst_ring = Ring(C, "kst", [128, 8, 512], BF16, 2)
    vst_ring = Ring(C, "vst", [128, 4, 512], BF16, 2)
    pv = pT_ap.rearrange("(c p) t -> p c t", p=128)
    qv = qB_ap.rearrange("h p t -> p h t")
    kv = kB_ap.rearrange("h p t -> p h t")
    vv = vB_ap.rearrange("(n p) c -> p n c", p=128)

    def load(t):
        ts = slice(t * 512, (t + 1) * 512)
        pos = (t * 512) % S
        cq, cqr = cq_ring.next()
        P.add("sp", lambda e: e.dma_start(out=cq[:, :, :], in_=pv[:, 8:14, ts]), reads=[p_res[t]], writes=[cqr], dma=cqr)
        ck, ckr = ckv_ring.next()
        P.add("sp", lambda e: e.dma_start(out=ck[:, :, :], in_=pv[:, 14:16, ts]), reads=[p_res[t]], writes=[ckr], dma=ckr)
        kp, kpr = kp_ring.next()
        P.add("sp", lambda e: e.dma_start(out=kp[64:96, :, :], in_=pv[0:32, 16:18, ts]), reads=[p_res[t]], writes=[kpr], dma=kpr)
        cs, csr = cs_ring.next()

        def lcs(e):
            return [e.dma_start(out=cs[64:96, 0, :], in_=cos_ap[:, pos:pos + 512]),
                    e.dma_start(out=cs[64:96, 1, :], in_=sin_ap[:, pos:pos + 512])]
        P.add("sp", lcs, writes=[csr], dma=csr, ndma=2)
        return cq, cqr, ck, ckr, kp, kpr, cs, csr
    nxt = load(0)
    for t in range(NT):
        cq, cqr, ck, ckr, kp, kpr, cs, csr = nxt
        if t + 1 < NT:
            nxt = load(t + 1)
        ts = slice(t * 512, (t + 1) * 512)
        sqq, sqqr = sqq_ring.next()
        P.add("act", lambda e, sqq=sqq, cq=cq: e.activation(out=sqq[:, :, :], in_=cq[:, :, :], func=AF.Square), reads=[cqr], writes=[sqqr])
        sqk, sqkr = sqk_ring.next()
        P.add("act", lambda e, sqk=sqk, ck=ck: e.activation(out=sqk[:, :, :], in_=ck[:, :, :], func=AF.Square), reads=[ckr], writes=[sqkr])
        ps1, p1r = C.psall.next()

        def mmq(e, ps1=ps1, sqq=sqq):
            last = None
            for k in range(6):
                last = e.matmul(ps1[:, :], lhsT=C.ones[:, :], rhs=sqq[:, k, :], start=(k == 0), stop=(k == 5))
            return last
        P.add("pe", mmq, reads=[sqqr, C.ones_r], writes=[p1r])
        rq, rqr = rq_ring.next()
        P.add("act", lambda e, rq=rq, ps1=ps1: e.activation(out=rq[:, :], in_=ps1[:, :], func=AF.Sqrt, bias=C.eps_t[:, 1:2], scale=0.125),
              reads=[p1r, C.ones_r], writes=[rqr])
        P.add("dve", lambda e, rq=rq: e.reciprocal(out=rq[:, :], in_=rq[:, :]), reads=[rqr], writes=[rqr])
        ps2, p2r = C.psall.next()

        def mmk(e, ps2=ps2, sqk=sqk):
            last = None
            for k in range(2):
                last = e.matmul(ps2[:, :], lhsT=C.ones[:, :], rhs=sqk[:, k, :], start=(k == 0), stop=(k == 1))
            return last
        P.add("pe", mmk, reads=[sqkr, C.ones_r], writes=[p2r])
        rk, rkr = rk_ring.next()
        P.add("act", lambda e, rk=rk, ps2=ps2: e.activation(out=rk[:, :], in_=ps2[:, :], func=AF.Sqrt, bias=C.eps_t[:, 0:1], scale=1.0 / 256),
              reads=[p2r, C.ones_r], writes=[rkr])
        P.add("dve", lambda e, rk=rk: e.reciprocal(out=rk[:, :], in_=rk[:, :]), reads=[rkr], writes=[rkr])
        ps3, p3r = C.psall.next()

        def mmkt(e, ps3=ps3, sqk=sqk):
            last = None
            for sl in range(4):
                for k in range(2):
                    last = e.matmul(ps3[:, sl:sl + 1], lhsT=sqk[:, k, sl * 128:(sl + 1) * 128], rhs=C.ones[:, 0:1],
                                    start=(k == 0), stop=(k == 1))
            return last
        P.add("pe", mmkt, reads=[sqkr, C.ones_r], writes=[p3r])
        rkt, rktr = rkt_ring.next()
        P.add("act", lambda e, rkt=rkt, ps3=ps3: e.activation(out=rkt[:, :], in_=ps3[:, 0:4], func=AF.Sqrt, bias=C.eps_t[:, 0:1], scale=1.0 / 256),
              reads=[p3r, C.ones_r], writes=[rktr])
        P.add("dve", lambda e, rkt=rkt: e.reciprocal(out=rkt[:, :], in_=rkt[:, :]), reads=[rktr], writes=[rktr])
        t3, t3r = t1_ring.next()
        t4, t4r = t2_ring.next()
        P.add("dve", lambda e, t3=t3, kp=kp, cs=cs: e.tensor_tensor(out=t3[64:96, :], in0=kp[64:96, 0, :], in1=cs[64:96, 0, :], op=ALU.mult),
              reads=[kpr, csr], writes=[t3r])
        P.add("dve", lambda e, t4=t4, kp=kp, cs=cs: e.tensor_tensor(out=t4[64:96, :], in0=kp[64:96, 1, :], in1=cs[64:96, 1, :], op=ALU.mult),
              reads=[kpr, csr], writes=[t4r])
        P.add("dve", lambda e, t3=t3, t4=t4: e.tensor_tensor(out=t3[64:96, :], in0=t3[64:96, :], in1=t4[64:96, :], op=ALU.add),
              reads=[t3r, t4r], writes=[t3r])
        qst, qsr = qst_ring.next()
        kst, ksr = kst_ring.next()

        def kcopy(e, kst=kst, t3=t3):
            last = None
            for h in range(8):
                last = e.activation(out=kst[64:96, h, :], in_=t3[64:96, :], func=AF.Copy)
            return last
        P.add("act", kcopy, reads=[t3r], writes=[ksr])
        for h in range(8):
            pa, par = C.psall.next()

            def mma(e, pa=pa, h=h, cq=cq):
                last = None
                for k in range(6):
                    last = e.matmul(pa[0:96, :], lhsT=Wq[:, k, h * 96:(h + 1) * 96], rhs=cq[:, k, :], start=(k == 0), stop=(k == 5))
                return last
            P.add("pe", mma, reads=[cqr, ar], writes=[par])
            P.add("dve", lambda e, qst=qst, h=h, pa=pa, rq=rq: e.tensor_tensor(out=qst[0:96, h, :], in0=pa[0:96, :], in1=rq[0:96, :], op=ALU.mult),
                  reads=[par, rqr], writes=[qsr])
            pb, pbr = C.psall.next()
            P.add("pe", lambda e, pb=pb, qst=qst, h=h: e.matmul(pb[0:96, :], lhsT=C.rotm[64:96, 0:96], rhs=qst[64:96, h, :], start=True, stop=True),
                  reads=[qsr, C.rotm_r], writes=[pbr])
            t1, t1r = t1_ring.next()
            t2, t2r = t2_ring.next()
            P.add("dve", lambda e, t1=t1, qst=qst, h=h, cs=cs: e.tensor_tensor(out=t1[64:96, :], in0=qst[64:96, h, :], in1=cs[64:96, 0, :], op=ALU.mult),
                  reads=[qsr, csr], writes=[t1r])
            P.add("dve", lambda e, t2=t2, pb=pb, cs=cs: e.tensor_tensor(out=t2[64:96, :], in0=pb[64:96, :], in1=cs[64:96, 1, :], op=ALU.mult),
                  reads=[pbr, csr], writes=[t2r])
            P.add("dve", lambda e, qst=qst, h=h, t1=t1, t2=t2: e.tensor_tensor(out=qst[64:96, h, :], in0=t1[64:96, :], in1=t2[64:96, :], op=ALU.add),
                  reads=[t1r, t2r, qsr], writes=[qsr])
            pk, pkr = C.psall.next()

            def mmkn(e, pk=pk, h=h, ck=ck):
                last = None
                for k in range(2):
                    last = e.matmul(pk[0:64, :], lhsT=Wkv[:, k, h * 128:h * 128 + 64], rhs=ck[:, k, :], start=(k == 0), stop=(k == 1))
                return last
            P.add("pe", mmkn, reads=[ckr, ar], writes=[pkr])
            P.add("dve", lambda e, kst=kst, h=h, pk=pk, rk=rk: e.tensor_tensor(out=kst[0:64, h, :], in0=pk[0:64, :], in1=rk[0:64, :], op=ALU.mult),
                  reads=[pkr, rkr], writes=[ksr])
        P.add("sp", lambda e, qst=qst, ts=ts: e.dma_start(out=qv[:, :, ts], in_=qst[0:96, :, :]), reads=[qsr], writes=[qk_res[t][0]], dma=qsr)
        P.add("sp", lambda e, kst=kst, ts=ts: e.dma_start(out=kv[:, :, ts], in_=kst[0:96, :, :]), reads=[ksr], writes=[qk_res[t][1]], dma=ksr)
        vst, vsr = vst_ring.next()
        Wv = Wkv.rearrange("p k (h d) -> p k h d", h=8)
        for sl in range(4):
            pvv, pvr = C.psall.next()

            def mmv(e, pvv=pvv, sl=sl, ck=ck):
                last = None
                for k in range(2):
                    last = e.matmul(pvv[:, :].rearrange("p (h d) -> p h d", h=8), lhsT=ck[:, k, sl * 128:(sl + 1) * 128],
                                    rhs=Wv[:, k, :, 64:128], start=(k == 0), stop=(k == 1))
                return last
            P.add("pe", mmv, reads=[ckr, ar], writes=[pvr])
            P.add("act", lambda e, vst=vst, sl=sl, pvv=pvv, rkt=rkt: e.activation(out=vst[:, sl, :], in_=pvv[:, :], func=AF.Copy, scale=rkt[:, sl:sl + 1]),
                  reads=[pvr, rktr], writes=[vsr])
        P.add("sp", lambda e, vst=vst, t=t: e.dma_start(out=vv[:, t * 4:(t + 1) * 4, :], in_=vst[:, :, :]), reads=[vsr], writes=[vb_res[t]], dma=vsr)


def finalize_aug(C, na, nar, ot, otr, rd_ring, bc_ring, ncols):
    P = C.P
    P.add("act", lambda e: e.activation(out=na[64:65, 0:ncols], in_=na[64:65, 0:ncols], func=AF.Ln), reads=[nar], writes=[nar])
    P.add("act", lambda e: e.activation(out=na[64:65, 0:ncols], in_=na[64:65, 0:ncols], func=AF.Exp, scale=-1.0), reads=[nar], writes=[nar])
    for c in range(ncols // 512):
        cs = slice(c * 512, (c + 1) * 512)
        ps, pr = C.psF.next()
        P.add("pe", lambda e, ps=ps, cs=cs: e.matmul(ps[0:64, :], lhsT=C.ones32[64:65, 0:64], rhs=na[64:65, cs], start=True, stop=True),
              reads=[nar, C.ones_r], writes=[pr])
        P.add("dve", lambda e, ps=ps, cs=cs: e.tensor_tensor(out=ot[0:64, cs], in0=na[0:64, cs], in1=ps[0:64, :], op=ALU.mult),
              reads=[nar, pr], writes=[otr])


def new_mix_res(P, mix_res, s):
    r = P.res("mix")
    mix_res[s].append(r)
    return r


def phase_mla_attn(C, qB_ap, kB_ap, vB_ap, qk_res, vb_res, mixT_ap, mix_res):
    P = C.P
    nseq = C.T // S
    C.ws_reset()
    q_ring = Ring(C, "mq", [128, S], BF16, 2)
    k_ring = Ring(C, "mk", [128, S], BF16, 2)
    v_ring = Ring(C, "mv", [128, 16, 8, 65], BF16, 2)
    pt_ring = Ring(C, "mpt", [128, 512], BF16, 4)
    na_ring = Ring(C, "mna", [65, S], F32, 2)
    o_ring = Ring(C, "mo", [64, S], BF16, 2)
    AS = AttnStream(C, pt_ring, L=3)
    for (vb, vbr) in v_ring.bufs:
        P.add("dve", lambda e, vb=vb: e.memset(vb[:, :, :, 64:65], 1.0), writes=[vbr])
    heads = [(s, h) for s in range(nseq) for h in range(8)]
    ld = {}
    vcur = {}

    def issue_loads(i):
        s, h = heads[i]
        tres = [r for t in range(s * 4, s * 4 + 4) for r in qk_res[t]]
        if h == 0:
            vb, vbr = v_ring.next()
            P.add("sp", lambda e: [e.dma_start(out=vb[:, :, hh, 0:64],
                                               in_=vB_ap[s * S:(s + 1) * S, hh * 64:(hh + 1) * 64].rearrange("(n p) d -> p n d", p=128)) for hh in range(8)],
                  reads=[vb_res[t] for t in range(s * 4, s * 4 + 4)], writes=[vbr], dma=vbr, ndma=8)
            vcur[s] = (vb, vbr)
        qt, qr = q_ring.next()
        kt, kr = k_ring.next()
        P.add("sp", lambda e: e.dma_start(out=qt[0:96, :], in_=qB_ap[h, :, s * S:(s + 1) * S]), reads=tres, writes=[qr], dma=qr)
        P.add("sp", lambda e: e.dma_start(out=kt[0:96, :], in_=kB_ap[h, :, s * S:(s + 1) * S]), reads=tres, writes=[kr], dma=kr)
        ld[i] = (qt, qr, kt, kr)

    def compute(i):
        s, h = heads[i]
        qt, qr, kt, kr = ld.pop(i)
        vb, vbr = vcur[s]
        na, nar = na_ring.next()
        hs = dict(left=4)

        def head_final():
            ot, otr = o_ring.next()
            finalize_aug(C, na, nar, ot, otr, None, None, S)
            P.add("sp", lambda e: e.dma_start(out=mixT_ap[512 + h * 64:512 + (h + 1) * 64, s * S:(s + 1) * S], in_=ot[0:64, :]),
                  reads=[otr], writes=[new_mix_res(P, mix_res, s)], dma=otr)
        for qg in range(4):
            def fin(acc, rd, qg=qg):
                P.add("dve", lambda e: e.tensor_copy(out=na[0:65, qg * 512:(qg + 1) * 512], in_=acc), reads=rd, writes=[nar])
                hs["left"] -= 1
                if hs["left"] == 0:
                    AS.defer(head_final, 6)
            g = AS.begin_group(512, qt[0:96, qg * 512:(qg + 1) * 512], [qr], 16, fin)
            for j in range(16):
                AS.pair(g, kt[0:96, j * 128:(j + 1) * 128], vb[:, j, h, :], None, [kr, vbr])
            if qg == 0 and i + 1 < len(heads):
                issue_loads(i + 1)
    issue_loads(0)
    for i in range(len(heads)):
        compute(i)
    AS.finish()


NEG = -30000.0


def sl_(start, n, step):
    return slice(start, start + (n - 1) * step + 1, step)
A_NT = 20
D_NT = 21
D_NP = 26


def _t5_bucket(rel):
    nb = 16
    max_exact = 8
    n = np.abs(rel)
    large = max_exact + (np.log(np.maximum(n, 1) / max_exact) / np.log(1024 / max_exact) * (nb - max_exact)).astype(np.int64)
    large = np.minimum(large, nb - 1)
    return ((rel > 0) * nb + np.where(n < max_exact, n, large)).astype(np.int32)


def build_bias_A(t5_bias):
    out = np.full((8, 128, A_NT, 128), NEG, np.float32)
    p = np.arange(128)[:, None]
    i = np.arange(128)[None, :]
    ti = 0
    for dil in (1, 4):
        for kind in ("lo_e", "hi", "lo", "hi", "lo", "hi", "lo", "hi_e"):
            if kind.startswith("lo"):
                off = p - 64 - i
                kvalid = (p >= 64) if kind == "lo_e" else np.ones_like(p, bool)
            else:
                off = p + 64 - i
                kvalid = (p < 64) if kind == "hi_e" else np.ones_like(p, bool)
            valid = (np.abs(off) <= 64) & kvalid
            b = t5_bias[_t5_bucket(off * dil)]
            for h in range(8):
                out[h, :, ti, :] = np.where(valid, b[:, :, h], NEG)
            ti += 1
    off = p - i
    valid = np.abs(off) <= 64
    b = t5_bias[_t5_bucket(off * 16)]
    for k in range(4):
        for h in range(8):
            out[h, :, ti, :] = np.where(valid, b[:, :, h], NEG)
        ti += 1
    assert ti == A_NT
    return np.ascontiguousarray(out.reshape(8, 128, A_NT * 128))


def a_tile_pos(dil_i, j, which, nblk):
    k = j // 2
    last = nblk // 2 - 1
    start = 0 if k == 0 else (4 if k == last else 2)
    return dil_i * 8 + start + (j % 2) * 2 + which


def d_block_tiles(m):
    if m <= 1:
        kts = [0, 1, 2, 3]
    elif m >= 14:
        kts = [12, 13, 14, 15]
    else:
        kts = [m - 2, m - 1, m, m + 1, m + 2]
    res = []
    for kt in kts:
        if m <= 1:
            idx = 5 + m * 4 + kt
        elif m >= 14:
            idx = 13 + (m - 14) * 4 + (kt - 12)
        else:
            idx = kt - m + 2
        res.append((kt, idx))
    return res


def build_bias_D(rpb):
    out = np.full((8, 128, D_NT, 128), NEG, np.float32)
    p = np.arange(128)[:, None]
    i = np.arange(128)[None, :]
    done = set()
    for m in list(range(16)):
        for kt, idx in d_block_tiles(m):
            if idx in done:
                continue
            if idx < 5 and not (2 <= m <= 13):
                continue
            done.add(idx)
            qi = 2 * m + i // 64
            qc = i % 64
            kr = 2 * kt + p // 64
            kc = p % 64
            r0 = np.clip(qi - 4, 0, 24)
            qs = np.clip(qc - 8, 0, 48)
            valid = (kr >= r0) & (kr < r0 + 8) & (kc >= qs) & (kc < qs + 16)
            dr = np.clip(kr - qi + 7, 0, 14)
            dc = np.clip(kc - qc + 15, 0, 30)
            for h in range(8):
                out[h, :, idx, :] = np.where(valid, rpb[h][dr, dc], NEG)
    assert len(done) == D_NT
    order = [5, 6, 7, 8, 9, 10, 11, 12, 0, 1, 2, 3, 4, 0, 1, 2, 3, 4, 13, 14, 15, 16, 17, 18, 19, 20]
    out2 = out[:, :, order, :]
    return np.ascontiguousarray(out2.reshape(8, 128, D_NP * 128))


def d_tile_pos(m, t):
    if m == 0:
        return t
    if m == 1:
        return 4 + t
    if m == 14:
        return 18 + t
    if m == 15:
        return 22 + t
    n = (m - 2) * 5 + t
    n0 = 4 * (n // 4)
    return 8 + (n0 % 5) + (n - n0)


def phase_attn_A(C, pT_ap, p_res, vtok_ap, v_res, biasA_ap, mixT_ap, mix_res):
    P = C.P
    nseq = C.T // S
    C.ws_reset()
    q_ring = Ring(C, "aq", [64, S], BF16, 2)
    k_ring = Ring(C, "ak", [64, 2 * S], BF16, 2)
    b_ring = Ring(C, "ab", [128, A_NT, 128], BF16, 2)
    v1_ring = Ring(C, "av1", [128, 17, 2, 65], BF16, 2)
    v4_ring = Ring(C, "av4", [128, 20, 2, 65], BF16, 2)
    v16_ring = Ring(C, "av16", [128, 16, 2, 65], BF16, 2)
    pt_ring = Ring(C, "apt", [128, 512], BF16, 4)
    na_ring = Ring(C, "ana", [65, S], F32, 2)
    o_ring = Ring(C, "ao", [64, S], BF16, 2)
    WARM_A = 0
    for (kt, kr) in k_ring.bufs:
        P.add("dve", lambda e, kt=kt: e.memset(kt[:, :], 0.0), writes=[kr])
    for ring in (v1_ring, v4_ring, v16_ring):
        for (vt, vr) in ring.bufs:
            P.add("dve", lambda e, vt=vt: e.memset(vt[:, :, :, 0:64], 0.0), writes=[vr])
            P.add("dve", lambda e, vt=vt: e.memset(vt[:, :, :, 64:65], 1.0), writes=[vr])
    AS = AttnStream(C, pt_ring, L=3, warm=WARM_A)
    heads = [(s, hg, hh) for s in range(nseq) for hg in range(4) for hh in range(2)]
    ld = {}
    vcur = {}

    def issue_loads(i):
        s, hg, hh = heads[i]
        h = hg * 2 + hh
        s0 = s * S
        tres = [p_res[t] for t in range(s * 4, s * 4 + 4)]
        vres = [v_res[t] for t in range(s * 4, s * 4 + 4)]
        qt, qr = q_ring.next()
        kt, kr = k_ring.next()
        bt, br = b_ring.next()
        P.add("sp", lambda e: e.dma_start(out=qt[:, :], in_=pT_ap[h * 64:(h + 1) * 64, s0:s0 + S]), reads=tres, writes=[qr], dma=qr)
        P.add("sp", lambda e: e.dma_start(out=kt[:, 1024:1024 + S], in_=pT_ap[512 + h * 64:512 + (h + 1) * 64, s0:s0 + S]), reads=tres, writes=[kr], dma=kr)
        P.add("pool", lambda e: e.dma_start(out=bt[:, :, :], in_=biasA_ap[h].rearrange("p (a b) -> p a b", a=A_NT)), writes=[br], dma=br)
        P.add("act", lambda e: e.activation(out=bt[:, :, :], in_=bt[:, :, :], func=AF.Exp), reads=[br], writes=[br])
        ld[i] = (qt, qr, kt, kr, bt, br)
        if hh == 0:
            v1, v1r = v1_ring.next()
            v4, v4r = v4_ring.next()
            v16, v16r = v16_ring.next()
            c0 = hg * 128

            def lv1(e):
                ins = []
                for x in range(2):
                    c = slice(c0 + x * 64, c0 + (x + 1) * 64)
                    ins.append(e.dma_start(out=v1[:, 1:16, x, 0:64], in_=vtok_ap[s0 + 64:s0 + 1984, c].rearrange("(n p) d -> p n d", p=128)))
                    ins.append(e.dma_start(out=v1[64:128, 0, x, 0:64], in_=vtok_ap[s0:s0 + 64, c]))
                    ins.append(e.dma_start(out=v1[0:64, 16, x, 0:64], in_=vtok_ap[s0 + 1984:s0 + 2048, c]))
                return ins
            P.add("sp", lv1, reads=vres, writes=[v1r], dma=v1r, ndma=6)

            def lv4(e):
                ins = []
                for x in range(2):
                    c = slice(c0 + x * 64, c0 + (x + 1) * 64)
                    vs = vtok_ap[s0:s0 + S, c].rearrange("(i r) c -> r i c", r=4)
                    for r in range(4):
                        ins.append(e.dma_start(out=v4[:, r * 5 + 1:r * 5 + 4, x, 0:64], in_=vs[r, 64:448, :].rearrange("(n p) d -> p n d", p=128)))
                        ins.append(e.dma_start(out=v4[64:128, r * 5, x, 0:64], in_=vs[r, 0:64, :]))
                        ins.append(e.dma_start(out=v4[0:64, r * 5 + 4, x, 0:64], in_=vs[r, 448:512, :]))
                return ins
            P.add("sp", lv4, reads=vres, writes=[v4r], dma=v4r, ndma=24)

            def lv16(e):
                return [e.dma_start(out=v16[:, :, x, 0:64],
                                    in_=vtok_ap[s0:s0 + S, c0 + x * 64:c0 + (x + 1) * 64].rearrange("(p r) d -> p r d", r=16)) for x in range(2)]
            P.add("sp", lv16, reads=vres, writes=[v16r], dma=v16r, ndma=2)
            vcur[(s, hg)] = (v1, v1r, v4, v4r, v16, v16r)

    def compute(i):
        s, hg, hh = heads[i]
        h = hg * 2 + hh
        s0 = s * S
        qt, qr, kt, kr, bt, br = ld.pop(i)
        v1, v1r, v4, v4r, v16, v16r = vcur[(s, hg)]
        na, nar = na_ring.next()
        hs = dict(left=48)

        def head_final():
            ot, otr = o_ring.next()
            finalize_aug(C, na, nar, ot, otr, None, None, S)
            P.add("sp", lambda e: e.dma_start(out=mixT_ap[h * 64:(h + 1) * 64, s0:s0 + S], in_=ot[:, :]),
                  reads=[otr], writes=[new_mix_res(P, mix_res, s)], dma=otr)

        def gdone():
            hs["left"] -= 1
            if hs["left"] == 0:
                AS.defer(head_final, 6)
        for j in range(16):
            def fin(acc, rd, j=j):
                P.add("dve", lambda e: e.tensor_copy(out=na[0:65, j * 128:(j + 1) * 128], in_=acc), reads=rd, writes=[nar])
                gdone()
            g = AS.begin_group(128, qt[:, j * 128:(j + 1) * 128], [qr], 2, fin)
            for wi, m in ((0, j), (1, j + 1)):
                st = 1024 + 128 * m - 64
                AS.pair(g, kt[:, st:st + 128], v1[:, m, hh, :], None, [kr, v1r], e=(bt, a_tile_pos(0, j, wi, 16), br))
        if i + 1 < len(heads):
            issue_loads(i + 1)
        for r in range(4):
            for j in range(4):
                q0 = r + 4 * 128 * j

                def fin(acc, rd, q0=q0):
                    P.add("dve", lambda e: e.tensor_tensor(out=na[0:65, sl_(q0, 128, 4)], in0=acc, in1=na[0:65, sl_(q0, 128, 4)], op=ALU.add),
                          reads=rd + [nar], writes=[nar])
                    gdone()
                g = AS.begin_group(128, qt[:, sl_(q0, 128, 4)], [qr], 2, fin)
                for wi, m in ((0, j), (1, j + 1)):
                    st = 1024 + r + 4 * (128 * m - 64)
                    AS.pair(g, kt[:, sl_(st, 128, 4)], v4[:, r * 5 + m, hh, :], None, [kr, v4r], e=(bt, a_tile_pos(1, j, wi, 4), br))
        for r in range(16):
            def fin(acc, rd, r=r):
                P.add("dve", lambda e: e.tensor_tensor(out=na[0:65, sl_(r, 128, 16)], in0=acc, in1=na[0:65, sl_(r, 128, 16)], op=ALU.add),
                      reads=rd + [nar], writes=[nar])
                gdone()
            g = AS.begin_group(128, qt[:, sl_(r, 128, 16)], [qr], 1, fin)
            AS.pair(g, kt[:, sl_(1024 + r, 128, 16)], v16[:, r, hh, :], None, [kr, v16r], e=(bt, 16 + (r % 4), br))
    issue_loads(0)
    for i in range(len(heads)):
        compute(i)
    AS.finish()


def phase_attn_D(C, pT_ap, p_res, vtok_ap, v_res, biasD_ap, mixT_ap, mix_res):
    P = C.P
    nseq = C.T // S
    C.ws_reset()
    q_ring = Ring(C, "dq", [64, S], BF16, 2)
    k_ring = Ring(C, "dk", [64, S], BF16, 2)
    b_ring = Ring(C, "db", [128, D_NP, 128], BF16, 2)
    v_ring = Ring(C, "dv", [128, 16, 8, 65], BF16, 2)
    pt_ring = Ring(C, "dpt", [128, 512], BF16, 4)
    na_ring = Ring(C, "dna", [65, S], F32, 2)
    o_ring = Ring(C, "do", [64, S], BF16, 2)
    AS = AttnStream(C, pt_ring, L=3, warm=0)
    for (vb, vbr) in v_ring.bufs:
        P.add("dve", lambda e, vb=vb: e.memset(vb[:, :, :, 64:65], 1.0), writes=[vbr])
    heads = [(s, h) for s in range(nseq) for h in range(8)]
    ld = {}
    vcur = {}

    def issue_loads(i):
        s, h = heads[i]
        s0 = s * S
        tres = [p_res[t] for t in range(s * 4, s * 4 + 4)]
        vres = [v_res[t] for t in range(s * 4, s * 4 + 4)]
        if h == 0:
            vb, vbr = v_ring.next()
            P.add("sp", lambda e: [e.dma_start(out=vb[:, :, x, 0:64],
                                               in_=vtok_ap[s0:s0 + S, x * 64:(x + 1) * 64].rearrange("(n p) d -> p n d", p=128)) for x in range(8)],
                  reads=vres, writes=[vbr], dma=vbr, ndma=8)
            vcur[s] = (vb, vbr)
        qt, qr = q_ring.next()
        kt, kr = k_ring.next()
        bt, br = b_ring.next()
        P.add("sp", lambda e: e.dma_start(out=qt[:, :], in_=pT_ap[h * 64:(h + 1) * 64, s0:s0 + S]), reads=tres, writes=[qr], dma=qr)
        P.add("sp", lambda e: e.dma_start(out=kt[:, :], in_=pT_ap[512 + h * 64:512 + (h + 1) * 64, s0:s0 + S]), reads=tres, writes=[kr], dma=kr)
        P.add("pool", lambda e: e.dma_start(out=bt[:, :, :], in_=biasD_ap[h].rearrange("p (a b) -> p a b", a=D_NP)), writes=[br], dma=br)
        P.add("act", lambda e: e.activation(out=bt[:, :, :], in_=bt[:, :, :], func=AF.Exp), reads=[br], writes=[br])
        ld[i] = (qt, qr, kt, kr, bt, br)

    def compute(i):
        s, h = heads[i]
        s0 = s * S
        qt, qr, kt, kr, bt, br = ld.pop(i)
        vb, vbr = vcur[s]
        na, nar = na_ring.next()
        hs = dict(left=16)

        def head_final():
            ot, otr = o_ring.next()
            finalize_aug(C, na, nar, ot, otr, None, None, S)
            P.add("sp", lambda e: e.dma_start(out=mixT_ap[512 + h * 64:512 + (h + 1) * 64, s0:s0 + S], in_=ot[:, :]),
                  reads=[otr], writes=[new_mix_res(P, mix_res, s)], dma=otr)
        for m in range(16):
            def fin(acc, rd, m=m):
                P.add("dve", lambda e: e.tensor_copy(out=na[0:65, m * 128:(m + 1) * 128], in_=acc), reads=rd, writes=[nar])
                hs["left"] -= 1
                if hs["left"] == 0:
                    AS.defer(head_final, 6)
            tiles = d_block_tiles(m)
            g = AS.begin_group(128, qt[:, m * 128:(m + 1) * 128], [qr], len(tiles), fin)
            for ti, (j, idx) in enumerate(tiles):
                AS.pair(g, kt[:, j * 128:(j + 1) * 128], vb[:, j, h, :], None, [kr, vbr], e=(bt, d_tile_pos(m, ti), br))
            if m == 5 and i + 1 < len(heads):
                issue_loads(i + 1)
    issue_loads(0)
    for i in range(len(heads)):
        compute(i)
    AS.finish()


def gelu_from_psum(C, ps, pr, out_ap, out_res, tmp_ring):
    C.P.add("act", lambda e: e.activation(out=out_ap, in_=ps[:, :], func=AF.Gelu_apprx_tanh), reads=[pr], writes=[out_res])


def phase_gating(C, h_ap, h_res, W, w_res, wsT_ap, bs_ap, vg_ap, mixT_ap, mix_res):
    P = C.P
    T = C.T
    NT = T // 512
    C.ws_reset()
    wsT, wsr = C.ws("wsT", [128, 8, 128], BF16)
    bsr_t, bsr = C.ws("bsrow", [1, 8, 128], BF16)
    vgn, vgr = C.ws("vgain", [128, 512], F32)
    P.add("pool", lambda e: e.dma_start(out=wsT[:, :, :], in_=wsT_ap.rearrange("p (g i) -> p g i", g=8)), writes=[wsr], dma=wsr)
    P.add("pool", lambda e: e.dma_start(out=bsr_t[:, :, :], in_=bs_ap.rearrange("p (g i) -> p g i", g=8)), writes=[bsr], dma=bsr)
    P.add("sp", lambda e: e.dma_start(out=vgn[:, :], in_=vg_ap), writes=[vgr], dma=vgr)
    x_ring = Ring(C, "gx", [128, 8, 512], BF16, 2)
    tmp_ring = Ring(C, "gt", [128, 512], F32, 3)
    u_ring = Ring(C, "gu", [128, 4, 512], BF16, 2)
    vg_ring = Ring(C, "gv", [128, 512], F32, 8)
    sq_ring = Ring(C, "gsq", [128, 512], F32, 2)
    st_ring = Ring(C, "gst", [128, 8, 4], F32, 2)
    st2_ring = Ring(C, "gs2", [128, 4], F32, 2)
    vn_ring = Ring(C, "gvn", [128, 512], BF16, 8)
    c_ring = Ring(C, "gc", [128, 4, 512], BF16, 2)
    hv = h_ap.rearrange("(c p) t -> p c t", p=128)
    mv = mixT_ap[0:512, :].rearrange("(c p) t -> p c t", p=128)

    def load(t):
        xt, xr = x_ring.next()
        P.add("sp", lambda e: e.dma_start(out=xt[:, :, :], in_=hv[:, :, t * 512:(t + 1) * 512]), reads=[h_res[t]], writes=[xr], dma=xr)
        return xt, xr
    nxt = load(0)
    for t in range(NT):
        xt, xr = nxt
        if t + 1 < NT:
            nxt = load(t + 1)
        ut, ur = u_ring.next()
        for cc in range(4):
            ps, pr = C.psall.next()

            def mm(e, ps=ps, cc=cc, xt=xt):
                last = None
                for k in range(8):
                    last = e.matmul(ps[:, :], lhsT=W[:, k, cc * 128:(cc + 1) * 128], rhs=xt[:, k, :], start=(k == 0), stop=(k == 7))
                return last
            P.add("pe", mm, reads=[xr, w_res], writes=[pr])
            gelu_from_psum(C, ps, pr, ut[:, cc, :], ur, tmp_ring)
        ct, cr = c_ring.next()
        vg = []
        for sl in range(4):
            ps, pr = C.psall.next()

            def mmv(e, ps=ps, sl=sl, xt=xt):
                last = None
                for k in range(8):
                    last = e.matmul(ps[:, :], lhsT=xt[:, k, sl * 128:(sl + 1) * 128], rhs=W[:, k, 512:1024], start=(k == 0), stop=(k == 7))
                return last
            P.add("pe", mmv, reads=[xr, w_res], writes=[pr])
            vgt, vgtr = vg_ring.next()
            gelu_from_psum(C, ps, pr, vgt[:, :], vgtr, tmp_ring)
            vg.append((vgt, vgtr))
        stt, str_ = st_ring.next()
        for sl in range(4):
            vgt, vgtr = vg[sl]
            P.add("dve", lambda e, vgt=vgt, sl=sl: e.reduce_sum(out=stt[:, 0, sl:sl + 1], in_=vgt[:, :], axis=mybir.AxisListType.X),
                  reads=[vgtr], writes=[str_])
        s2t, s2r = st2_ring.next()
        for sl in range(4):
            vgt, vgtr = vg[sl]
            sq, sqr = sq_ring.next()
            P.add("act", lambda e, sq=sq, vgt=vgt, sl=sl: e.activation(out=sq[:, :], in_=vgt[:, :], func=AF.Square, accum_out=s2t[:, sl:sl + 1]),
                  reads=[vgtr], writes=[sqr, s2r])
        P.add("dve", lambda e: e.tensor_scalar(out=stt[:, 2, :], in0=stt[:, 0, :], scalar1=-1.0 / 512, scalar2=None, op0=ALU.mult),
              reads=[str_], writes=[str_])
        P.add("dve", lambda e: e.tensor_tensor(out=stt[:, 3, :], in0=stt[:, 2, :], in1=stt[:, 2, :], op=ALU.mult),
              reads=[str_], writes=[str_])
        P.add("dve", lambda e: e.scalar_tensor_tensor(out=stt[:, 4, :], in0=s2t[:, 0:4], scalar=1.0 / 512, in1=stt[:, 3, :],
                                                      op0=ALU.mult, op1=ALU.subtract), reads=[str_, s2r], writes=[str_])
        P.add("act", lambda e: e.activation(out=stt[:, 5, :], in_=stt[:, 4, :], func=AF.Sqrt, bias=C.eps_t[:, 0:1], scale=1.0),
              reads=[str_, C.ones_r], writes=[str_])
        P.add("dve", lambda e: e.reciprocal(out=stt[:, 5, :], in_=stt[:, 5, :]), reads=[str_], writes=[str_])
        vns = []
        for sl in range(4):
            vgt, vgtr = vg[sl]
            P.add("dve", lambda e, vgt=vgt, sl=sl: e.tensor_scalar(out=vgt[:, :], in0=vgt[:, :], scalar1=stt[:, 2, sl:sl + 1], scalar2=stt[:, 5, sl:sl + 1],
                                                               op0=ALU.add, op1=ALU.mult), reads=[str_, vgtr], writes=[vgtr])
            vn, vnr = vn_ring.next()
            P.add("pool", lambda e, vn=vn, vgt=vgt: e.tensor_tensor(out=vn[:, :], in0=vgt[:, :], in1=vgn[:, :], op=ALU.mult),
                  reads=[vgtr, vgr], writes=[vnr])
            vns.append((vn, vnr))
        for sl in range(4):
            vn, vnr = vns[sl]
            for cc in range(4):
                pg, pgr = C.psall.next()

                def mmg(e, pg=pg, cc=cc, vn=vn):
                    last = None
                    for half in range(2):
                        g = 2 * cc + half
                        e.matmul(pg[:, half * 128:(half + 1) * 128], lhsT=vn[:, cc * 128:(cc + 1) * 128], rhs=wsT[:, g, :], start=True, stop=False)
                        last = e.matmul(pg[:, half * 128:(half + 1) * 128], lhsT=C.ones[0:1, :], rhs=bsr_t[0:1, g, :], start=False, stop=True)
                    return last
                P.add("pe", mmg, reads=[vnr, wsr, bsr, C.ones_r], writes=[pgr])

                def ev(e, pg=pg, cc=cc, sl=sl, ct=ct, ut=ut):
                    e.tensor_tensor(out=ct[0:64, cc, sl * 128:(sl + 1) * 128], in0=pg[0:64, 0:128], in1=ut[0:64, cc, sl * 128:(sl + 1) * 128], op=ALU.mult)
                    return e.tensor_tensor(out=ct[64:128, cc, sl * 128:(sl + 1) * 128], in0=pg[64:128, 128:256], in1=ut[64:128, cc, sl * 128:(sl + 1) * 128], op=ALU.mult)
                P.add("dve", ev, reads=[pgr, ur], writes=[cr])
        P.add("sp", lambda e, ct=ct, t=t: e.dma_start(out=mv[:, :, t * 512:(t + 1) * 512], in_=ct[:, :, :]),
              reads=[cr], writes=[new_mix_res(P, mix_res, t // 4)], dma=cr)


VEC_QG = 72
VEC_KVG = 88
NVEC = 96


def build_program(T, depth=4):
    nc = bass.Bass("TRN2", target_bir_lowering=False)

    def inp(name, shape):
        return nc.dram_tensor(name, list(shape), F32, kind="ExternalInput").ap()
    x_in = inp("x_in", [D, T])
    vecs = inp("vecs", [128, NVEC])
    ident_in = inp("ident_in", [128, 128])
    rotm_in = inp("rotm_in", [128, 96])
    w_in_e = inp("w_in_e", [2, D, 2592])
    w_uq = inp("w_uq", [2, 768, 768])
    w_ukv = inp("w_ukv", [2, 256, 1024])
    w_out_e = inp("w_out_e", [2, D, D])
    w_in_o = inp("w_in_o", [2, D, 2560])
    wsT_in = inp("wsT", [2, 128, 1024])
    bs_in = inp("bs", [2, 1, 1024])
    vgain_in = inp("vgain", [2, 128, 512])
    w_out_o = inp("w_out_o", [2, D, D])
    wgu = inp("wgu", [4, D, 2 * FH])
    wd = inp("wd", [4, FH, D])
    biasA = inp("biasA", [8, 128, A_NT * 128])
    biasD = inp("biasD", [2, 8, 128, D_NP * 128])
    cos_in = inp("cos", [32, S])
    sin_in = inp("sin", [32, S])
    out = nc.dram_tensor("out", [D, T], F32, kind="ExternalOutput").ap()
    with ExitStack() as stack:
        C = Ctx(nc, stack, T)
        P = C.P
        NT = T // 512
        nseq = T // S
        P.add("pool", lambda e: e.dma_start(out=C.ident[:, :], in_=ident_in[:, :]), writes=[C.ident_r], dma=C.ident_r)
        P.add("pool", lambda e: e.dma_start(out=C.rotm[:, :], in_=rotm_in[:, :]), writes=[C.rotm_r], dma=C.rotm_r)
        vec_sb = stack.enter_context(nc.sbuf_tensor("vec_sb", [128, NVEC], F32))
        vec_res = P.res("vec")
        P.add("sp", lambda e: e.dma_start(out=vec_sb[:, :], in_=vecs[:, :]), writes=[vec_res], dma=vec_res)
        xT = C.dram("xT", [D, T], F32)
        hT = C.dram("hT", [D, T], BF16)
        pT = C.dram("pT", [18 * 128, T], BF16)
        vtok = C.dram("vtok", [T, 512], BF16)
        mixT = C.dram("mixT", [D, T], BF16)
        aT = C.dram("aT", [FH, T], BF16)
        qB = C.dram("qB", [8, 96, T], BF16)
        kB = C.dram("kB", [8, 96, T], BF16)
        vB = C.dram("vB", [T, 512], BF16)
        xin_res = [P.res("xin") for _ in range(NT)]
        x_res = [P.res("x") for _ in range(NT)]
        h_res = [P.res("h") for _ in range(NT)]
        p_res = [P.res("p") for _ in range(NT)]
        v_res = [P.res("v") for _ in range(NT)]
        a_res = [[P.res("a") for _ in range(NT)] for _ in range(2)]
        qk_res = [[P.res("q"), P.res("k")] for _ in range(NT)]
        vb_res = [P.res("vb") for _ in range(NT)]
        o_res = [P.res("o") for _ in range(NT)]

        def norm(src_ap, src_res, gcol, dst_ap, dst_res, dt):
            C.ws_reset()
            rn = (Ring(C, "xf", [128, 8, 512], F32, 3), Ring(C, "sq", [128, 8, 512], BF16, 2),
                  Ring(C, "rs", [128, 512], F32, 2), Ring(C, "hb", [128, 8, 512], dt, 2))
            phase_norm(C, src_ap, src_res, vec_sb, vec_res, gcol, dst_ap, dst_res, dt, rn)

        def inproj(W, w_res, fm, tm):
            C.ws_reset()
            rings = (Ring(C, "ix", [128, 8, 512], BF16, 2), Ring(C, "ist", [128, 18, 512], BF16, 2), Ring(C, "ivt", [128, 4, 512], BF16, 2))
            phase_inproj(C, hT, h_res, W, w_res, fm, tm, pT, p_res, vtok, v_res, rings)

        def outproj(in_ap, in_res_fn, K, w_ap, xs_ap, xs_res, norm_gcol=None):
            C.ws_reset()
            KC = K // 128
            rd = (Ring(C, "inb%d" % KC, [128, KC, 512], BF16, 2), Ring(C, "xr", [128, 8, 512], F32, 3))
            fn = None
            if norm_gcol is not None:
                r2 = (Ring(C, "fsq", [128, 8, 512], BF16, 1), Ring(C, "frs", [128, 512], F32, 2), Ring(C, "fhb", [128, 8, 512], BF16, 1))
                fn = (vec_sb, vec_res, norm_gcol, hT, h_res, r2)
            phase_proj_residual(C, in_ap, in_res_fn, K, w_ap, xs_ap, xs_res, xT, x_res, rd, fuse_norm=fn)

        xs_ap, xs_res = x_in, xin_res
        for l in range(depth):
            j = l // 2
            mix_res = [[] for _ in range(nseq)]
            if l % 2 == 0:
                arena = C.next_arena()
                W = C.load_weight(arena, w_in_e[j], D, [(0, 2592)], extra=32)

                def rotk(e, W=W):
                    e.tensor_scalar(out=W[:, :, 2592:2608], in0=W[:, :, 2576:2592], scalar1=-1.0, scalar2=None, op0=ALU.mult)
                    return e.tensor_copy(out=W[:, :, 2608:2624], in_=W[:, :, 2560:2576])
                P.add("dve", rotk, reads=[arena[1]], writes=[arena[1]])
                if l == 0:
                    norm(xs_ap, xs_res, 16 * l, hT, h_res, BF16)
                fm = [(g * 128, 128, 0.125) for g in range(4)] + [(512 + g * 128, 128, 1.0) for g in range(4)]
                fm += [(1536 + g * 128, 128, 1.0) for g in range(8)]
                fm += [(2560, 32, 1.0), (2592, 32, 1.0)]
                inproj(W, arena[1], fm, 1024)
                phase_attn_A(C, pT, p_res, vtok, v_res, biasA, mixT, mix_res)
                phase_mla_prep(C, pT, p_res, w_uq[j], w_ukv[j], vec_sb, vec_res, VEC_QG + 8 * j, VEC_KVG + 2 * j,
                               cos_in, sin_in, qB, kB, vB, qk_res, vb_res)
                phase_mla_attn(C, qB, kB, vB, qk_res, vb_res, mixT, mix_res)
                outproj(mixT, lambda t, mix_res=mix_res: mix_res[t // 4], D, w_out_e[j], xs_ap, xs_res, norm_gcol=16 * l + 8)
            else:
                arena = C.next_arena()
                W = C.load_weight(arena, w_in_o[j], D, [(0, 2560)])
                phase_gating(C, hT, h_res, W, arena[1], wsT_in[j], bs_in[j], vgain_in[j], mixT, mix_res)
                fm = [(1024 + g * 128, 128, 0.125) for g in range(4)] + [(1536 + g * 128, 128, 1.0) for g in range(4)]
                inproj(W, arena[1], fm, 2048)
                phase_attn_D(C, pT, p_res, vtok, v_res, biasD[j], mixT, mix_res)
                outproj(mixT, lambda t, mix_res=mix_res: mix_res[t // 4], D, w_out_o[j], xs_ap, xs_res, norm_gcol=16 * l + 8)
            xs_ap, xs_res = xT, x_res
            C.ws_reset()
            rg = (Ring(C, "hx", [128, 8, 512], BF16, 2), Ring(C, "sg", [128, 512], F32, 3), Ring(C, "ab", [128, 11, 512], BF16, 2))
            phase_ffn_gu(C, hT, h_res, wgu[l], aT, a_res, rg)
            outproj(aT, lambda t: [a_res[0][t], a_res[1][t]], FH, wd[l], xT, x_res,
                    norm_gcol=(16 * (l + 1) if l + 1 < depth else None))
        norm(xT, x_res, 64, out, o_res, F32)
        P.emit(final_waits=o_res)
    return nc


def _rot_matrix():
    r = np.zeros((128, 96), np.float32)
    for i in range(16):
        r[80 + i, 64 + i] = -1.0
        r[64 + i, 80 + i] = 1.0
    return r


def host_prep(inputs):
    f = lambda a: np.ascontiguousarray(np.asarray(a, dtype=np.float32))
    vecs = np.zeros((128, NVEC), np.float32)
    nm, nf = f(inputs["norm_mix"]), f(inputs["norm_ffn"])
    for l in range(4):
        vecs[:, 16 * l:16 * l + 8] = nm[l].reshape(8, 128).T
        vecs[:, 16 * l + 8:16 * l + 16] = nf[l].reshape(8, 128).T
    vecs[:, 64:72] = f(inputs["final_gain"]).reshape(8, 128).T
    qg, kvg = f(inputs["ev_q_gain"]), f(inputs["ev_kv_gain"])
    for j in range(2):
        vecs[:, VEC_QG + 8 * j:VEC_QG + 8 * j + 6] = qg[j].reshape(6, 128).T
        vecs[:, VEC_KVG + 2 * j:VEC_KVG + 2 * j + 2] = kvg[j].reshape(2, 128).T
    pos = np.arange(S, dtype=np.float32)
    inv = (1.0 / (np.float32(10000.0) ** (np.arange(0, 32, 2, dtype=np.float32) / np.float32(32)))).astype(np.float32)
    ang = pos[None, :] * inv[:, None]
    cos = np.cos(ang).astype(np.float32)
    sin = np.sin(ang).astype(np.float32)
    ws = f(inputs["od_w_s"])
    rpb = f(inputs["od_rpb"])
    shared = {
        "vecs": vecs,
        "ident_in": np.eye(128, dtype=np.float32),
        "rotm_in": _rot_matrix(),
        "w_in_e": f(inputs["ev_w_in"]), "w_uq": f(inputs["ev_w_uq"]), "w_ukv": f(inputs["ev_w_ukv"]),
        "w_out_e": f(inputs["ev_w_out"]), "w_in_o": f(inputs["od_w_in"]),
        "wsT": np.ascontiguousarray(ws.transpose(0, 3, 1, 2).reshape(2, 128, 1024)),
        "bs": f(inputs["od_b_s"]).reshape(2, 1, 1024),
        "vgain": np.ascontiguousarray(np.broadcast_to(f(inputs["od_v_gain"])[:, None, :], (2, 128, 512))),
        "w_out_o": f(inputs["od_w_out"]), "wgu": f(inputs["ffn_w_gu"]), "wd": f(inputs["ffn_w_down"]),
        "biasA": build_bias_A(f(inputs["t5_bias"])),
        "biasD": np.stack([build_bias_D(rpb[j]) for j in range(2)]),
        "cos": np.ascontiguousarray(np.concatenate([cos, cos], 0)),
        "sin": np.ascontiguousarray(np.concatenate([sin, sin], 0)),
    }
    return shared


def run_cores(x, shared, ncores, depth=4):
    B = x.shape[0]
    per = B // ncores
    T = per * S
    nc = build_program(T, depth)
    in_maps = []
    for c in range(ncores):
        xc = x[c * per:(c + 1) * per].reshape(T, D)
        m = dict(shared)
        m["x_in"] = np.ascontiguousarray(xc.T)
        in_maps.append(m)
    res = run_bass_kernel_spmd(nc, in_maps, core_ids=list(range(ncores)))
    outs = [np.asarray(r["out"]).T.reshape(per, S, D) for r in res.results]
    return np.ascontiguousarray(np.concatenate(outs, 0).astype(np.float32))


def kernel(**inputs):
    x = np.asarray(inputs["x"], dtype=np.float32)
    shared = host_prep(inputs)
    return run_cores(x, shared, NCORES)
```
